# Optimizing a Trainium2 kernel written in Bass

```python
import math
import jax, jax.numpy as jnp
from jax import lax
import numpy as np

D_MODEL = 2048
BATCH = 4
SEQ = 8192
DEPTH = 1

HEAD_DIM = 128
DIL_GROUPS = ((128, 1), (512, 4), (2048, 16))
DIL_HEADS_PER_GROUP = 8
DIL_N_HEADS = DIL_HEADS_PER_GROUP * len(DIL_GROUPS)
DIL_QKV = DIL_N_HEADS * HEAD_DIM
DIL_WIDTH = DIL_HEADS_PER_GROUP * HEAD_DIM
DIFF_N_HEADS = 6
DIFF_HEAD_DIM = 2 * HEAD_DIM
DIFF_WIDTH = DIFF_N_HEADS * DIFF_HEAD_DIM
Q_BLOCK = 128
EPS = 1e-6
NEG_INF = -1e30
IN_SIZES = (DIL_QKV, DIL_QKV, DIL_QKV, DIL_WIDTH,
            DIFF_WIDTH, DIFF_WIDTH, DIFF_WIDTH, DIFF_WIDTH,
            D_MODEL, D_MODEL)
N_IN = sum(IN_SIZES)
IN_SPLITS = [int(v) for v in np.cumsum(IN_SIZES)[:-1]]

kernel_name = "hybrid_dilated_diff_attn_gated_merge"


def rmsnorm(x, g):
    xf = x.astype(jnp.float32)
    y = xf * lax.rsqrt(jnp.mean(xf * xf, axis=-1, keepdims=True) + EPS)
    return y.astype(x.dtype) * g


def alibi_slopes(n):
    return jnp.asarray(2.0 ** (-8.0 * np.arange(1, n + 1) / n), dtype=jnp.float32)


def lambda_init_fn(layer):
    return 0.8 - 0.6 * math.exp(-0.3 * layer)


def dilated_group_attention(q, k, v, slopes, window, dilation):
    b, s, h, dh = q.shape
    r = dilation
    span = window // dilation
    L = s // r
    qb = span
    nb = -(-L // qb)
    lp = nb * qb
    qr, kr, vr = (t.reshape(b, L, r, h, dh) for t in (q, k, v))
    qp = jnp.pad(qr, ((0, 0), (0, lp - L), (0, 0), (0, 0), (0, 0))).reshape(b, nb, qb, r, h, dh)

    def band(t):
        tp = jnp.pad(t, ((0, 0), (qb, lp - L), (0, 0), (0, 0), (0, 0)))
        prev = tp[:, :lp].reshape(b, nb, qb, r, h, dh)
        cur = tp[:, qb:qb + lp].reshape(b, nb, qb, r, h, dh)
        return jnp.concatenate([prev, cur], axis=2)

    kb, vb = band(kr), band(vr)
    scores = jnp.einsum('bnqrhd,bnkrhd->bnrhqk', qp, kb).astype(jnp.float32) * (dh ** -0.5)
    delta = qb + jnp.arange(qb)[:, None] - jnp.arange(2 * qb)[None, :]
    key_u = (jnp.arange(nb)[:, None] - 1) * qb + jnp.arange(2 * qb)[None, :]
    valid = ((delta >= 0) & (delta <= span))[None] & (key_u >= 0)[:, None, :]
    bias = -slopes[:, None, None] * (r * delta).astype(jnp.float32)
    logits = jnp.where(valid[None, :, None, None], scores + bias[None, None, None], NEG_INF)
    lse = jax.nn.logsumexp(logits, axis=-1)
    p = jnp.exp(logits - lse[..., None])
    o = jnp.einsum('bnrhqk,bnkrhd->bnqrhd', p.astype(v.dtype), vb)
    o = o.reshape(b, lp, r, h, dh)[:, :L].reshape(b, s, h, dh)
    lse = lse.transpose(0, 1, 4, 2, 3).reshape(b, lp, r, h)[:, :L].reshape(b, s, h)
    return o, lse


def dilated_mixer(q, k, v, slopes):
    b, s, _ = q.shape
    ng = len(DIL_GROUPS)
    q, k, v = (t.reshape(b, s, ng, DIL_HEADS_PER_GROUP, HEAD_DIM) for t in (q, k, v))
    sl = slopes.reshape(ng, DIL_HEADS_PER_GROUP)
    outs, lses = [], []
    for g, (window, dilation) in enumerate(DIL_GROUPS):
        o, lse = dilated_group_attention(q[:, :, g], k[:, :, g], v[:, :, g], sl[g], window, dilation)
        outs.append(o)
        lses.append(lse)
    wgt = jax.nn.softmax(jnp.stack(lses, axis=0), axis=0)
    o = jnp.sum(wgt[..., None] * jnp.stack(outs, axis=0).astype(jnp.float32), axis=0)
    return o.reshape(b, s, DIL_WIDTH).astype(q.dtype)


def diff_attention(q, k, v, slopes, lam, g_subln, lambda_init):
    b, s, h, _, dh = q.shape
    nq = s // Q_BLOCK
    qblocks = q.reshape(b, nq, Q_BLOCK, h, 2, dh).transpose(1, 0, 2, 3, 4, 5)
    tk = jnp.arange(s)

    def one_block(args):
        qblk, n = args
        sc = jnp.einsum('bqhcd,bkhcd->bhcqk', qblk, k).astype(jnp.float32) * (dh ** -0.5)
        dist = n * Q_BLOCK + jnp.arange(Q_BLOCK)[:, None] - tk[None, :]
        logits = sc - slopes[None, :, None, None, None] * dist.astype(jnp.float32)[None, None, None]
        logits = jnp.where((dist >= 0)[None, None, None], logits, NEG_INF)
        p = jax.nn.softmax(logits, axis=-1)
        a = p[:, :, 0] - lam * p[:, :, 1]
        return jnp.einsum('bhqk,bkhe->bqhe', a.astype(v.dtype), v)

    o = lax.map(one_block, (qblocks, jnp.arange(nq)))
    o = o.transpose(1, 0, 2, 3, 4).reshape(b, s, h, 2 * dh)
    o = rmsnorm(o, g_subln) * (1.0 - lambda_init)
    return o.reshape(b, s, h * 2 * dh)


def setup_inputs(seed: int = 0) -> dict:
    key = jax.random.key(seed)
    ks = jax.random.split(key, 15)
    f32 = jnp.float32
    nrm = lambda k, shp, sc: jax.random.normal(k, shp, f32) * sc
    return {
        "x": nrm(ks[0], (BATCH, SEQ, D_MODEL), 1.0),
        "c": nrm(ks[1], (BATCH, D_MODEL), 1.0),
        "w_ada": nrm(ks[2], (DEPTH, D_MODEL, 3 * D_MODEL), 0.5 * D_MODEL ** -0.5),
        "b_ada": nrm(ks[3], (DEPTH, 3 * D_MODEL), 0.01),
        "g_norm": 1.0 + nrm(ks[4], (DEPTH, D_MODEL), 0.02),
        "w_in": nrm(ks[5], (DEPTH, D_MODEL, N_IN), D_MODEL ** -0.5),
        "w_o_dil": nrm(ks[6], (DEPTH, DIL_WIDTH, D_MODEL), DIL_WIDTH ** -0.5),
        "w_o_diff": nrm(ks[7], (DEPTH, DIFF_WIDTH, D_MODEL), DIFF_WIDTH ** -0.5),
        "w_out": nrm(ks[8], (DEPTH, D_MODEL, D_MODEL), D_MODEL ** -0.5),
        "lambda_q1": nrm(ks[9], (DEPTH, HEAD_DIM), 0.1),
        "lambda_k1": nrm(ks[10], (DEPTH, HEAD_DIM), 0.1),
        "lambda_q2": nrm(ks[11], (DEPTH, HEAD_DIM), 0.1),
        "lambda_k2": nrm(ks[12], (DEPTH, HEAD_DIM), 0.1),
        "g_subln": 1.0 + nrm(ks[13], (DEPTH, DIFF_HEAD_DIM), 0.02),
        "g_final": 1.0 + nrm(ks[14], (D_MODEL,), 0.02),
    }


def reference(x, c, w_ada, b_ada, g_norm, w_in, w_o_dil, w_o_diff, w_out,
              lambda_q1, lambda_k1, lambda_q2, lambda_k2, g_subln, g_final):
    b, s, _ = x.shape
    slopes_dil = alibi_slopes(DIL_N_HEADS)
    slopes_diff = alibi_slopes(DIFF_N_HEADS)
    for l in range(DEPTH):
        ada = jax.nn.silu(c) @ w_ada[l] + b_ada[l]
        shift, scale, gate = jnp.split(ada, 3, axis=-1)
        hn = rmsnorm(x, g_norm[l]) * (1.0 + scale[:, None]) + shift[:, None]
        proj = hn @ w_in[l]
        q_a, k_a, v_a, z_a, q_b, k_b, v_b, z_b, gl_a, gl_b = jnp.split(proj, IN_SPLITS, axis=-1)
        y_a = dilated_mixer(q_a, k_a, v_a, slopes_dil) * jax.nn.silu(z_a)
        lam_init = lambda_init_fn(l)
        lam = (jnp.exp(jnp.sum(lambda_q1[l] * lambda_k1[l]).astype(jnp.float32))
               - jnp.exp(jnp.sum(lambda_q2[l] * lambda_k2[l]).astype(jnp.float32)) + lam_init)
        y_b = diff_attention(q_b.reshape(b, s, DIFF_N_HEADS, 2, HEAD_DIM),
                             k_b.reshape(b, s, DIFF_N_HEADS, 2, HEAD_DIM),
                             v_b.reshape(b, s, DIFF_N_HEADS, DIFF_HEAD_DIM),
                             slopes_diff, lam, g_subln[l], lam_init).astype(x.dtype)
        y_b = y_b * jax.nn.silu(z_b)
        merged = jax.nn.sigmoid(gl_a) * (y_a @ w_o_dil[l]) + jax.nn.sigmoid(gl_b) * (y_b @ w_o_diff[l])
        x = x + gate[:, None] * (merged @ w_out[l])
    return rmsnorm(x, g_final)
```

```python
import math
from contextlib import ExitStack

import numpy as np
import ml_dtypes

import concourse.bass as bass
import concourse.mybir as mybir
from concourse.bass_utils import run_bass_kernel_spmd

F32 = mybir.dt.float32
BF16 = mybir.dt.bfloat16
AF = mybir.ActivationFunctionType
ALU = mybir.AluOpType
AX = mybir.AxisListType

D = 2048
KC = 16
S_LOC = 8192
S_OWN = 4096
CTX = 4096
N_IN = 20480
C_QA, C_KA, C_VA, C_ZA = 0, 3072, 6144, 9216
C_QB, C_KB, C_VB, C_ZB = 10240, 11776, 13312, 14848
C_GLA, C_GLB = 16384, 18432
EPS = 1e-6
NEG = -30000.0
DIL = ((128, 1), (512, 4), (2048, 16))
SC = 128 ** -0.5
LAM_INIT = 0.8 - 0.6 * math.exp(0.0)
STOP = None


class Sem:
    def __init__(self, h):
        self.h = h
        self.n = 0


class Tok:
    __slots__ = ("sem", "val")

    def __init__(self, sem=None, val=None):
        self.sem = sem
        self.val = val


class Buf:
    def __init__(self, name):
        self.name = name
        self.w = None
        self.r = []
        self.dsem = None


class Prog:
    ENG = ("sync", "scalar", "gpsimd", "vector", "tensor")

    def __init__(self, nc, stack):
        self.nc = nc
        self.st = stack
        self.q = {e: [] for e in self.ENG}
        self.waited = {e: {} for e in self.ENG}
        self.cnt = {}
        for e in ("scalar", "gpsimd", "vector", "tensor"):
            self.cnt[e] = self.sem("c_" + e)
        self.pe_pending = []
        self.nsem = 4

    def sem(self, name):
        return Sem(self.st.enter_context(self.nc.semaphore(name)))

    def sbuf(self, name, shape, dt):
        return self.st.enter_context(self.nc.sbuf_tensor(name, shape, dt))

    def psum(self, name, shape, dt):
        return self.st.enter_context(self.nc.psum_tensor(name, shape, dt))

    def _waits(self, eng, reads, writes):
        toks = []
        for b in reads:
            if b.w is not None:
                toks.append(b.w)
        for b in writes:
            if b.w is not None:
                toks.append(b.w)
            toks.extend(b.r)
        out = {}
        pec = self.cnt["tensor"]
        for t in toks:
            if eng == "tensor" and t.sem is pec:
                continue
            assert t.val is not None, "unresolved PE read token"
            k = id(t.sem)
            if k not in out or out[k][1] < t.val:
                out[k] = (t.sem, t.val)
        res = []
        wd = self.waited[eng]
        for k, (s, v) in out.items():
            if wd.get(k, 0) >= v:
                continue
            wd[k] = v
            res.append((s, v))
        return res

    def _commit(self, tok, reads, writes):
        for b in reads:
            b.r.append(tok)
        for b in writes:
            b.w = tok
            b.r = []

    def op(self, eng, f, reads=(), writes=()):
        waits = self._waits(eng, reads, writes)
        c = self.cnt[eng]
        c.n += 1
        tok = Tok(c, c.n)

        def run(e, f=f, waits=waits, c=c):
            for s, v in waits:
                e.wait_ge(s.h, v)
            f(e).then_inc(c.h, 1)
        self.q[eng].append(run)
        self._commit(tok, reads, writes)
        return tok

    def pe(self, f, reads=(), writes=(), mark=False):
        waits = self._waits("tensor", reads, writes)
        c = self.cnt["tensor"]
        if mark:
            c.n += 1
            tok = Tok(c, c.n)
            for t in self.pe_pending:
                t.sem, t.val = c, c.n
            self.pe_pending = []
        else:
            tok = Tok(c, None)
            self.pe_pending.append(tok)

        def run(e, f=f, waits=waits, c=c, mark=mark):
            for s, v in waits:
                e.wait_ge(s.h, v)
            ins = f(e)
            if mark:
                ins.then_inc(c.h, 1)
        self.q["tensor"].append(run)
        self._commit(tok, reads, writes)
        return tok

    def dma(self, eng, f, reads=(), writes=(), sem_buf=None):
        waits = self._waits(eng, reads, writes)
        if sem_buf.dsem is None:
            sem_buf.dsem = self.sem("d_" + sem_buf.name)
            self.nsem += 1
        s = sem_buf.dsem
        s.n += 16
        tok = Tok(s, s.n)

        def run(e, f=f, waits=waits, s=s):
            for sm, v in waits:
                e.wait_ge(sm.h, v)
            f(e).then_inc(s.h, 16)
        self.q[eng].append(run)
        self._commit(tok, reads, writes)
        return tok

    def wait_tok(self, eng, tok):
        if tok is None:
            return
        wd = self.waited[eng]
        k = id(tok.sem)
        if wd.get(k, 0) >= tok.val:
            return
        wd[k] = tok.val
        self.q[eng].append(lambda e, s=tok.sem, v=tok.val: e.wait_ge(s.h, v))

    def emit(self):
        with self.nc.Block() as block:
            for name in self.ENG:
                fns = self.q[name]
                if not fns:
                    continue

                def body(e, fns=fns):
                    for fn in fns:
                        fn(e)
                getattr(block, name)(body)


class Ring:
    def __init__(self, P, name, n, shape, dt, psum=False):
        alloc = P.psum if psum else P.sbuf
        self.t = [alloc(f"{name}{i}", shape, dt) for i in range(n)]
        self.b = [Buf(f"{name}{i}") for i in range(n)]
        self.i = 0
        self.n = n

    def next(self):
        i = self.i
        self.i = (i + 1) % self.n
        return self.t[i], self.b[i]


class Arena:
    def __init__(self, P, words):
        self.t = P.sbuf("arena", [128, words], F32)
        self.words = words
        self.off = 0
        self.k = 0

    def f32(self, n, name=None):
        assert self.off + n <= self.words, ("SBUF arena overflow", self.off, n)
        ap = self.t[:, self.off:self.off + n]
        self.off += n
        self.k += 1
        return ap, Buf(name or f"a{self.k}")

    def bf16(self, n, name=None):
        assert n % 2 == 0
        ap, b = self.f32(n // 2, name)
        return ap.bitcast(BF16), b

    def ring_f32(self, n, k, name):
        return RingA([self.f32(n, f"{name}{i}") for i in range(k)])

    def ring_bf16(self, n, k, name):
        return RingA([self.bf16(n, f"{name}{i}") for i in range(k)])


class RingA:
    def __init__(self, items):
        self.items = items
        self.i = 0

    def next(self):
        it = self.items[self.i]
        self.i = (self.i + 1) % len(self.items)
        return it


def build_program():
    nc = bass.Bass("TRN2", target_bir_lowering=False)

    def din(name, shape, dt=F32):
        return nc.dram_tensor(name, list(shape), dt, kind="ExternalInput").ap()

    def dscr(name, shape, dt):
        return nc.dram_tensor(name, list(shape), dt).ap()

    x = din("x", [S_LOC, D])
    c_in = din("c", [D])
    w_ada = din("w_ada", [D, 3 * D])
    b_ada = din("b_ada", [3 * D])
    g_norm = din("g_norm", [D])
    w_in = din("w_in", [D, N_IN])
    w_od = din("w_o_dil", [1024, D])
    w_of = din("w_o_diff", [1536, D])
    w_out = din("w_out", [D, D])
    lamv = din("lamv", [512])
    g_sub = din("g_subln", [256])
    g_fin = din("g_final", [D])
    dbias_in = din("dbias", [128, 2 * 6 * 67])
    tri_in = din("tri", [128, 128])
    ident_in = din("ident", [128, 128])
    mdil_in = din("mdil", [128, 24 * 3 * 128])
    y_out = nc.dram_tensor("y", [S_OWN, D], F32, kind="ExternalOutput").ap()

    wbf_in = dscr("wbf_in", [D, N_IN], BF16)
    wbf_od = dscr("wbf_od", [1024, D], BF16)
    wbf_of = dscr("wbf_of", [1536, D], BF16)
    wbf_out = dscr("wbf_out", [D, D], BF16)
    QaT = dscr("QaT", [24, 128, S_OWN], BF16)
    KaT = dscr("KaT", [24, 128, 6144], BF16)
    Va = dscr("Va", [6144, 3072], BF16)
    Za = dscr("Za", [S_OWN, 1024], F32)
    QbT = dscr("QbT", [12, 128, S_OWN], BF16)
    KbT = dscr("KbT", [12, 128, S_LOC], BF16)
    Vb = dscr("Vb", [S_LOC, 1536], BF16)
    Zb = dscr("Zb", [S_OWN, 1536], F32)
    GL = dscr("GL", [32, 128, S_OWN], F32)
    ND = dscr("ND", [3, S_OWN, 8 * 129], F32)
    YT = dscr("YT", [20, 128, S_OWN], BF16)

    with ExitStack() as st:
        P = Prog(nc, st)
        ar = Arena(P, 51200)
        PS = [P.psum(f"ps{i}", [128, 1024], F32) for i in range(4)]
        psb = [[Buf(f"ps{i}_{h}") for h in range(2)] for i in range(4)]

        def bank(i):
            return PS[i // 2][:, (i % 2) * 512:(i % 2) * 512 + 512], psb[i // 2][i % 2]

        scratch_bufs = []

        def barrier():
            toks = []
            for e in ("scalar", "gpsimd", "vector", "tensor"):
                c = P.cnt[e]
                if c.n > 0:
                    toks.append(Tok(c, c.n))
            assert not P.pe_pending
            for b in all_dma_bufs:
                if b.dsem is not None and b.dsem.n > 0:
                    toks.append(Tok(b.dsem, b.dsem.n))
            for e in P.ENG:
                for t in toks:
                    P.wait_tok(e, t)

        all_dma_bufs = []

        def dbuf(b):
            if b not in all_dma_bufs:
                all_dma_bufs.append(b)
            return b

        def dma(eng, out, in_, reads=(), writes=(), sem_buf=None, **kw):
            dbuf(sem_buf)
            return P.dma(eng, lambda e: e.dma_start(out=out, in_=in_, **kw), reads=reads, writes=writes,
                         sem_buf=sem_buf)

        A_rep, A_b = ar.f32(D, "A_rep")
        B_rep, B_b = ar.f32(D, "B_rep")
        G_rep, G_b = ar.f32(D, "G_rep")
        GF_rep, GF_b = ar.f32(D, "GF_rep")
        ident, ident_b = ar.bf16(128, "ident")
        tri, tri_b = ar.bf16(128, "tri")
        small, small_b = ar.f32(64, "small")
        gs_rep, gs_b = ar.f32(256, "gs_rep")
        neglam = small[:, 0:1]
        eps_c = small[:, 8:9]
        persist_off = ar.off

        dma("gpsimd", ident, ident_in, writes=[ident_b], sem_buf=ident_b)
        dma("gpsimd", tri, tri_in, writes=[tri_b], sem_buf=tri_b)
        dma("sync", GF_rep, g_fin.partition_broadcast(128), writes=[GF_b], sem_buf=GF_b)
        dma("sync", gs_rep, g_sub.partition_broadcast(128), writes=[gs_b], sem_buf=gs_b)
        P.op("vector", lambda e: e.tensor_scalar(out=gs_rep, in0=gs_rep, scalar1=float(1.0 - LAM_INIT), scalar2=None,
                                                 op0=ALU.mult), reads=[], writes=[gs_b])
        P.op("vector", lambda e: e.memset(eps_c, EPS), writes=[small_b])
        L, L_b = ar.f32(512, "lamL")
        dma("sync", L, lamv.partition_broadcast(128), writes=[L_b], sem_buf=L_b)
        lp, lp_b = ar.f32(256, "lamP")
        P.op("vector", lambda e: e.tensor_tensor(out=lp[:, 0:128], in0=L[:, 0:128], in1=L[:, 128:256], op=ALU.mult),
             reads=[L_b], writes=[lp_b])
        P.op("vector", lambda e: e.tensor_tensor(out=lp[:, 128:256], in0=L[:, 256:384], in1=L[:, 384:512], op=ALU.mult),
             reads=[L_b], writes=[lp_b])
        P.op("vector", lambda e: e.reduce_sum(out=small[:, 1:2], in_=lp[:, 0:128], axis=AX.X), reads=[lp_b],
             writes=[small_b])
        P.op("vector", lambda e: e.reduce_sum(out=small[:, 2:3], in_=lp[:, 128:256], axis=AX.X), reads=[lp_b],
             writes=[small_b])
        P.op("scalar", lambda e: e.activation(out=small[:, 3:5], in_=small[:, 1:3], func=AF.Exp), reads=[small_b],
             writes=[small_b])
        P.op("vector", lambda e: e.scalar_tensor_tensor(out=neglam, in0=small[:, 4:5], scalar=float(-LAM_INIT),
                                                        in1=small[:, 3:4], op0=ALU.add, op1=ALU.subtract),
             reads=[small_b], writes=[small_b])

        cs, cs_b = ar.f32(16, "cs")
        dma("sync", cs, c_in.rearrange("(kc p) -> p kc", p=128), writes=[cs_b], sem_buf=cs_b,
            allow_slow_non_contiguous=True)
        P.op("scalar", lambda e: e.activation(out=cs, in_=cs, func=AF.Silu), reads=[cs_b], writes=[cs_b])
        ones, ones_b = ar.f32(128, "ones")
        P.op("vector", lambda e: e.memset(ones, 1.0), writes=[ones_b])
        screp, screp_b = ar.f32(16 * 128, "screp")
        for kc in range(KC):
            P.op("scalar", lambda e, kc=kc: e.activation(out=screp[:, kc * 128:(kc + 1) * 128], in_=ones, func=AF.Copy,
                                                         scale=cs[:, kc:kc + 1]), reads=[cs_b, ones_b], writes=[screp_b])
        ada, ada_b = ar.f32(3 * D, "ada")
        dma("sync", ada, b_ada.partition_broadcast(128), writes=[ada_b], sem_buf=ada_b)
        g_rep, g_rep_b = ar.f32(D, "g_rep")
        dma("sync", g_rep, g_norm.partition_broadcast(128), writes=[g_rep_b], sem_buf=g_rep_b)
        wa_ring = ar.ring_f32(16 * 512, 2, "wada")
        for cb in range(12):
            wt, wt_b = wa_ring.next()
            dma("sync", wt.rearrange("p (k n) -> p k n", k=KC),
                w_ada[:, cb * 512:(cb + 1) * 512].rearrange("(kc p) n -> p kc n", p=128), writes=[wt_b], sem_buf=wt_b)
            pa_, pb_ = bank(cb % 2)
            for kc in range(KC):
                P.pe(lambda e, kc=kc, wt=wt, pa_=pa_: e.matmul(pa_, lhsT=screp[:, kc * 128:(kc + 1) * 128],
                                                                rhs=wt[:, kc * 512:(kc + 1) * 512], start=(kc == 0),
                                                                stop=(kc == KC - 1)),
                     reads=[screp_b, wt_b], writes=[pb_], mark=(kc == KC - 1))
            P.op("vector", lambda e, cb=cb, pa_=pa_: e.tensor_tensor(out=ada[:, cb * 512:(cb + 1) * 512], in0=pa_,
                                                                      in1=ada[:, cb * 512:(cb + 1) * 512], op=ALU.add),
                 reads=[pb_], writes=[ada_b])
        P.op("vector", lambda e: e.scalar_tensor_tensor(out=A_rep, in0=ada[:, D:2 * D], scalar=1.0, in1=g_rep, op0=ALU.add,
                                                        op1=ALU.mult), reads=[ada_b, g_rep_b], writes=[A_b])
        P.op("vector", lambda e: e.tensor_copy(out=B_rep, in_=ada[:, 0:D]), reads=[ada_b], writes=[B_b])
        P.op("vector", lambda e: e.tensor_copy(out=G_rep, in_=ada[:, 2 * D:3 * D]), reads=[ada_b], writes=[G_b])

        wb = Buf("wcast")
        for i in range(16):
            dma("gpsimd", wbf_in[i * 128:(i + 1) * 128, :], w_in[i * 128:(i + 1) * 128, :], sem_buf=wb)
        for (dst, src, rows) in ((wbf_od, w_od, 1024), (wbf_of, w_of, 1536), (wbf_out, w_out, 2048)):
            for i in range(rows // 512):
                dma("gpsimd", dst[i * 512:(i + 1) * 512, :], src[i * 512:(i + 1) * 512, :], sem_buf=wb)
        barrier()
        ar.off = persist_off
        if STOP == 'W':
            P.emit()
            return nc

        H, H_b = ar.bf16(KC * 2048, "H")
        H3 = H.rearrange("p (k t) -> p k t", k=KC)
        xt_ring = ar.ring_f32(D, 2, "xt")
        hn_ring = ar.ring_bf16(D, 2, "hn")
        wt_ring = ar.ring_bf16(KC * 512, 3, "wt")
        og_ring = ar.ring_f32(512, 4, "og")
        ss_ring = ar.ring_f32(2, 4, "ss")
        np_off = ar.off
        scr = Buf("scr_np")
        bank_i = [0]

        def next_bank(lo=0, hi=8):
            i = lo + bank_i[0] % (hi - lo)
            bank_i[0] += 1
            return bank(i)

        def norm_chunk(T0):
            for tt in range(16):
                xt, xt_b = xt_ring.next()
                hn, hn_b = hn_ring.next()
                ss, ss_b = ss_ring.next()
                t0 = T0 + tt * 128
                dma("sync", xt, x[t0:t0 + 128, :], writes=[xt_b], sem_buf=xt_b)
                P.op("vector", lambda e, ss=ss: e.memset(ss, 0.0), writes=[ss_b])
                P.op("scalar", lambda e, xt=xt, hn=hn, ss=ss: e.activation(out=hn, in_=xt, func=AF.Square,
                                                                            accum_out=ss[:, 0:1]),
                     reads=[xt_b], writes=[hn_b, ss_b])
                P.op("scalar", lambda e, ss=ss: e.activation(out=ss[:, 1:2], in_=ss[:, 0:1], func=AF.Ln, scale=1.0 / D,
                                                             bias=eps_c), reads=[ss_b, small_b], writes=[ss_b])
                P.op("scalar", lambda e, ss=ss: e.activation(out=ss[:, 1:2], in_=ss[:, 1:2], func=AF.Exp, scale=-0.5),
                     reads=[ss_b], writes=[ss_b])
                P.op("vector", lambda e, xt=xt, ss=ss: e.scalar_tensor_tensor(out=xt, in0=xt, scalar=ss[:, 1:2], in1=A_rep,
                                                                              op0=ALU.mult, op1=ALU.mult),
                     reads=[xt_b, ss_b, A_b], writes=[xt_b])
                P.op("gpsimd", lambda e, xt=xt, hn=hn: e.tensor_tensor(out=hn, in0=xt, in1=B_rep, op=ALU.add),
                     reads=[xt_b, B_b], writes=[hn_b])
                pi = (tt % 2) * 2
                ptile = PS[pi // 2][:, :].bitcast(BF16)
                pbufs = [psb[pi // 2][0], psb[pi // 2][1]]
                for kc in range(KC):
                    P.pe(lambda e, kc=kc, hn=hn, ptile=ptile: e.transpose(ptile[:, kc * 128:(kc + 1) * 128],
                                                                          hn[:, kc * 128:(kc + 1) * 128], ident),
                         reads=[hn_b, ident_b], writes=pbufs, mark=(kc == KC - 1))
                P.op("scalar", lambda e, tt=tt, ptile=ptile: e.copy(out=H3[:, :, tt * 128:(tt + 1) * 128],
                                                                    in_=ptile.rearrange("p (k t) -> p k t", k=KC)),
                     reads=pbufs, writes=[H_b])

        def load_w(col0):
            wt, wt_b = wt_ring.next()
            dma("sync", wt.rearrange("p (k n) -> p k n", k=KC),
                wbf_in[:, col0:col0 + 512].rearrange("(kc p) n -> p kc n", p=128), writes=[wt_b], sem_buf=wt_b)
            return wt.rearrange("p (k n) -> p k n", k=KC), wt_b

        def proj_fm(col0, tgs, sink):
            wt, wt_b = load_w(col0)
            for cbl in range(4):
                for tg in tgs:
                    pa_, pb_ = next_bank(4, 8)
                    for kc in range(KC):
                        P.pe(lambda e, kc=kc, wt=wt, cbl=cbl, tg=tg, pa_=pa_: e.matmul(
                            pa_, lhsT=wt[:, kc, cbl * 128:(cbl + 1) * 128], rhs=H3[:, kc, tg * 512:(tg + 1) * 512],
                            start=(kc == 0), stop=(kc == KC - 1)), reads=[wt_b, H_b], writes=[pb_], mark=(kc == KC - 1))
                    sink(col0 // 128 + cbl, tg, pa_, pb_)

        def proj_tm(col0, tts, sink):
            wt, wt_b = load_w(col0)
            for tt in tts:
                pa_, pb_ = next_bank(4, 8)
                for kc in range(KC):
                    P.pe(lambda e, kc=kc, wt=wt, tt=tt, pa_=pa_: e.matmul(
                        pa_, lhsT=H3[:, kc, tt * 128:(tt + 1) * 128], rhs=wt[:, kc, :],
                        start=(kc == 0), stop=(kc == KC - 1)), reads=[wt_b, H_b], writes=[pb_], mark=(kc == KC - 1))
                sink(col0, tt, pa_, pb_)

        def evac_store(pa_, pb_, dst, func, dt):
            og, og_b = og_ring.next()
            ov = og if dt == F32 else og[:, 0:256].bitcast(BF16)
            P.op("scalar", lambda e: e.activation(out=ov, in_=pa_, func=func), reads=[pb_], writes=[og_b])
            dma("scalar", dst, ov, reads=[og_b], writes=[], sem_buf=og_b)

        def sink_fm(dst_t, blk0, tok_of_tg, func=AF.Copy, dt=BF16):
            def sink(cb, tg, pa_, pb_):
                t0 = tok_of_tg(tg)
                evac_store(pa_, pb_, dst_t[cb - blk0, :, t0:t0 + 512], func, dt)
            return sink

        def sink_tm(dst_t, c0, row_of_tt, func=AF.Copy, dt=BF16):
            def sink(col0, tt, pa_, pb_):
                r0 = row_of_tt(tt)
                evac_store(pa_, pb_, dst_t[r0:r0 + 128, col0 - c0:col0 - c0 + 512], func, dt)
            return sink

        for ch in range(4):
            T0 = ch * 2048
            norm_chunk(T0)
            alltg = list(range(4))
            alltt = list(range(16))
            for c0 in range(C_KB, C_KB + 1536, 512):
                proj_fm(c0, alltg, sink_fm(KbT, C_KB // 128, lambda tg: T0 + tg * 512))
            for c0 in range(C_VB, C_VB + 1536, 512):
                proj_tm(c0, alltt, sink_tm(Vb, C_VB, lambda tt: T0 + tt * 128))
            if ch >= 1:
                for g in range(3):
                    if ch == 1:
                        tgs = alltg if g == 2 else [3]
                    else:
                        tgs = alltg
                    tts = [t for t in alltt if t // 4 in tgs]
                    for c0 in range(C_KA + g * 1024, C_KA + (g + 1) * 1024, 512):
                        proj_fm(c0, tgs, sink_fm(KaT, C_KA // 128, lambda tg: T0 - 2048 + tg * 512))
                    for c0 in range(C_VA + g * 1024, C_VA + (g + 1) * 1024, 512):
                        proj_tm(c0, tts, sink_tm(Va, C_VA, lambda tt: T0 - 2048 + tt * 128))
            if ch >= 2:
                O0 = T0 - CTX
                for c0 in range(C_QA, C_QA + 3072, 512):
                    proj_fm(c0, alltg, sink_fm(QaT, C_QA // 128, lambda tg: O0 + tg * 512))
                for c0 in range(C_QB, C_QB + 1536, 512):
                    proj_fm(c0, alltg, sink_fm(QbT, C_QB // 128, lambda tg: O0 + tg * 512))
                for c0 in range(C_GLA, C_GLA + 4096, 512):
                    proj_fm(c0, alltg, sink_fm(GL, C_GLA // 128, lambda tg: O0 + tg * 512, func=AF.Sigmoid, dt=F32))
                for c0 in range(C_ZA, C_ZA + 1024, 512):
                    proj_tm(c0, alltt, sink_tm(Za, C_ZA, lambda tt: O0 + tt * 128, func=AF.Silu, dt=F32))
                for c0 in range(C_ZB, C_ZB + 1536, 512):
                    proj_tm(c0, alltt, sink_tm(Zb, C_ZB, lambda tt: O0 + tt * 128, func=AF.Silu, dt=F32))
        barrier()
        ar.off = persist_off
        if STOP == 'NP':
            P.emit()
            return nc

        mdil, mdil_b = ar.f32(24 * 3 * 128, "mdil")
        dma("sync", mdil, mdil_in, writes=[mdil_b], sem_buf=mdil_b)
        kt_ring = ar.ring_bf16(6144, 2, "dkt")
        qt_ring = ar.ring_bf16(4096, 2, "dqt")
        v_ring = ar.ring_bf16(48 * 130, 2, "dv")
        e_ring = ar.ring_f32(256, 3, "dE")
        pt_ring = ar.ring_bf16(256, 4, "dPT")
        o_ring = ar.ring_f32(130, 4, "dO")
        for (vv, vv_b) in v_ring.items:
            P.op("gpsimd", lambda e, vv=vv: e.memset(vv.rearrange("p (n c) -> p n c", c=130)[:, :, 128:130], 1.0),
                 writes=[vv_b])
        for hh in range(24):
            g, slot = hh // 8, hh % 8
            r = DIL[g][1]
            nbp = 1 + 32 // r
            kt, kt_b = kt_ring.next()
            qt, qt_b = qt_ring.next()
            vv, vv_b = v_ring.next()
            v3 = vv.rearrange("p (n c) -> p n c", c=130)
            klen = 128 * r + S_OWN
            dma("sync", kt[:, 0:klen], KaT[hh, :, 6144 - klen:6144], writes=[kt_b], sem_buf=kt_b)
            dma("sync", qt, QaT[hh, :, :], writes=[qt_b], sem_buf=qt_b)
            for p in range(r):
                r0 = 2048 - 128 * r + p
                src = Va[r0:r0 + r * (nbp * 128 - 1) + 1:r, hh * 128:(hh + 1) * 128]
                dma("sync", v3[:, p * nbp:(p + 1) * nbp, 0:128], src.rearrange("(n j) d -> j n d", j=128),
                    writes=[vv_b], sem_buf=vv_b)
            for p in range(r):
                for n in range(1, nbp):
                    first = (n == 1)
                    mA, mB = (0, 1) if first else (n, n - 1)
                    i0 = 0 if first else 1

                    def cols(m):
                        a = m * 128 * r + p
                        return slice(a, a + 127 * r + 1, r)
                    qa = (n - 1) * 128 * r + p
                    qcols = slice(qa, qa + 127 * r + 1, r)
                    ps_, psb_ = next_bank(0, 8)
                    P.pe(lambda e, ps_=ps_, kt=kt, qt=qt, c=cols(mA), qcols=qcols: e.matmul(
                        ps_[:, 0:128], lhsT=kt[:, c], rhs=qt[:, qcols], start=True, stop=True),
                        reads=[kt_b, qt_b], writes=[psb_])
                    P.pe(lambda e, ps_=ps_, kt=kt, qt=qt, c=cols(mB), qcols=qcols: e.matmul(
                        ps_[:, 128:256], lhsT=kt[:, c], rhs=qt[:, qcols], start=True, stop=True),
                        reads=[kt_b, qt_b], writes=[psb_], mark=True)
                    ee, ee_b = e_ring.next()
                    P.op("scalar", lambda e, ee=ee, ps_=ps_: e.activation(out=ee, in_=ps_[:, 0:256], func=AF.Exp, scale=SC),
                         reads=[psb_], writes=[ee_b])
                    pt, pt_b = pt_ring.next()
                    mo = (hh * 3 + i0) * 128
                    P.op("vector", lambda e, pt=pt, ee=ee, mo=mo: e.tensor_tensor(out=pt, in0=ee, in1=mdil[:, mo:mo + 256],
                                                                                   op=ALU.mult),
                         reads=[ee_b, mdil_b], writes=[pt_b])
                    ac_, acb_ = next_bank(0, 8)
                    P.pe(lambda e, ac_=ac_, pt=pt, v3=v3, bi=p * nbp + mA: e.matmul(
                        ac_[:, 0:129], lhsT=pt[:, 0:128], rhs=v3[:, bi, 0:129], start=True, stop=False),
                        reads=[pt_b, vv_b], writes=[acb_])
                    P.pe(lambda e, ac_=ac_, pt=pt, v3=v3, bi=p * nbp + mB: e.matmul(
                        ac_[:, 0:129], lhsT=pt[:, 128:256], rhs=v3[:, bi, 0:129], start=False, stop=True),
                        reads=[pt_b, vv_b], writes=[acb_], mark=True)
                    oo, oo_b = o_ring.next()
                    P.op("vector", lambda e, oo=oo, ac_=ac_: e.tensor_copy(out=oo[:, 0:129], in_=ac_[:, 0:129]),
                         reads=[acb_], writes=[oo_b])
                    dst = ND[g, qa:qa + 127 * r + 1:r, slot * 129:(slot + 1) * 129]
                    dma("gpsimd", dst, oo[:, 0:129], reads=[oo_b], sem_buf=oo_b)
        barrier()
        ar.off = persist_off
        nd_ring = ar.ring_f32(3 * 1032, 2, "nd")
        za_ring = ar.ring_f32(1024, 2, "za")
        ya_ring = ar.ring_bf16(1024, 2, "ya")
        yt_ring = ar.ring_bf16(1024, 2, "yt")
        rd_ring = ar.ring_f32(8, 2, "rd")
        for tt in range(32):
            t0 = tt * 128
            nd, nd_b = nd_ring.next()
            za, za_b = za_ring.next()
            ya, ya_b = ya_ring.next()
            yt, yt_b = yt_ring.next()
            rd, rd_b = rd_ring.next()
            nd3 = nd.rearrange("p (g c) -> p g c", g=3)
            dma("sync", nd3, ND[:, t0:t0 + 128, :].rearrange("g t c -> t g c"), writes=[nd_b], sem_buf=nd_b)
            dma("sync", za, Za[t0:t0 + 128, :], writes=[za_b], sem_buf=za_b)
            P.op("vector", lambda e, nd3=nd3: e.tensor_tensor(out=nd3[:, 0, :], in0=nd3[:, 0, :], in1=nd3[:, 1, :], op=ALU.add),
                 reads=[nd_b], writes=[nd_b])
            P.op("vector", lambda e, nd3=nd3: e.tensor_tensor(out=nd3[:, 0, :], in0=nd3[:, 0, :], in1=nd3[:, 2, :], op=ALU.add),
                 reads=[nd_b], writes=[nd_b])
            s3 = nd[:, 0:1032].rearrange("p (s c) -> p s c", c=129)
            P.op("vector", lambda e, rd=rd, s3=s3: e.reciprocal(out=rd.rearrange("p (s o) -> p s o", o=1), in_=s3[:, :, 128:129]),
                 reads=[nd_b], writes=[rd_b])
            y3 = nd[:, 1032:1032 + 1024].rearrange("p (s c) -> p s c", c=128)
            P.op("vector", lambda e, rd=rd, s3=s3, y3=y3: e.tensor_tensor(
                out=y3, in0=s3[:, :, 0:128], in1=rd.rearrange("p (s o) -> p s o", o=1).to_broadcast([128, 8, 128]), op=ALU.mult),
                reads=[nd_b, rd_b], writes=[nd_b])
            P.op("vector", lambda e, nd=nd, za=za, ya=ya: e.tensor_tensor(out=ya, in0=nd[:, 1032:1032 + 1024], in1=za, op=ALU.mult),
                 reads=[nd_b, za_b], writes=[ya_b])
            ps_, psb_ = next_bank(0, 8)
            pv = ps_.bitcast(BF16)
            for k in range(8):
                P.pe(lambda e, k=k, pv=pv, ya=ya: e.transpose(pv[:, k * 128:(k + 1) * 128], ya[:, k * 128:(k + 1) * 128], ident),
                     reads=[ya_b, ident_b], writes=[psb_], mark=(k == 7))
            P.op("scalar", lambda e, yt=yt, pv=pv: e.copy(out=yt, in_=pv), reads=[psb_], writes=[yt_b])
            dma("scalar", YT[0:8, :, t0:t0 + 128].rearrange("k p t -> p k t"), yt.rearrange("p (k t) -> p k t", k=8),
                reads=[yt_b], sem_buf=yt_b)
        barrier()
        ar.off = persist_off
        if STOP == 'DIL':
            P.emit()
            return nc

        dbias, dbias_b = ar.f32(2 * 6 * 67, "dbias")
        dma("sync", dbias, dbias_in, writes=[dbias_b], sem_buf=dbias_b)
        kt2_ring = ar.ring_bf16(2 * S_LOC, 2, "kt2")
        qt2_ring = ar.ring_bf16(2 * S_OWN, 2, "qt2")
        vd, vd_b = ar.bf16(64 * 258, "vd")
        vd3 = vd.rearrange("p (n c) -> p n c", c=258)
        P.op("gpsimd", lambda e: e.memset(vd3[:, :, 256:258], 1.0), writes=[vd_b])
        ptd_ring = ar.ring_bf16(512, 4, "ptd")
        ot_ring = ar.ring_f32(258, 8, "ot")
        zb_ring = ar.ring_f32(4 * 256, 2, "zb")
        rc_ring = ar.ring_f32(8, 4, "rc")
        yb_ring = ar.ring_bf16(256, 3, "yb")
        ybt_ring = ar.ring_bf16(256, 3, "ybt")
        for h in range(6):
            G = 2 if h == 0 else 4
            NQ = 128 * G
            kt2, kt2_b = kt2_ring.next()
            qt2, qt2_b = qt2_ring.next()
            k2 = kt2.rearrange("p (c t) -> p c t", c=2)
            q2 = qt2.rearrange("p (c t) -> p c t", c=2)
            for c in range(2):
                dma("sync", k2[:, c, :], KbT[h * 2 + c, :, :], writes=[kt2_b], sem_buf=kt2_b)
                dma("sync", q2[:, c, :], QbT[h * 2 + c, :, :], writes=[qt2_b], sem_buf=qt2_b)
            for q4 in range(4):
                dma("sync", vd3[:, q4 * 16:(q4 + 1) * 16, 0:256],
                    Vb[q4 * 2048:(q4 + 1) * 2048, h * 256:(h + 1) * 256].rearrange("(n j) e -> j n e", j=128),
                    writes=[vd_b], sem_buf=vd_b)
            for qg in range(32 // G):
                qb0 = qg * G
                zb, zb_b = zb_ring.next()
                zb3 = zb[:, 0:G * 256].rearrange("p (g e) -> p g e", g=G)
                dma("sync", zb3, Zb[qb0 * 128:(qb0 + G) * 128, h * 256:(h + 1) * 256].rearrange("(g j) e -> j g e", j=128),
                    writes=[zb_b], sem_buf=zb_b)
                P.op("gpsimd", lambda e, zb3=zb3, G=G: e.tensor_tensor(
                    out=zb3, in0=zb3, in1=gs_rep.rearrange("p (o e) -> p o e", o=1).to_broadcast([128, G, 256]), op=ALU.mult),
                    reads=[zb_b, gs_b], writes=[zb_b])
                ots = [[None] * G for _ in range(2)]
                for c in range(2):
                    kb_last = 32 + qb0 + G - 1
                    accs = [bank(qi) for qi in range(G)]
                    for kb in range(kb_last + 1):
                        ps_, psb_ = next_bank(4, 8)
                        P.pe(lambda e, ps_=ps_, k2=k2, q2=q2, c=c, kb=kb, qb0=qb0, NQ=NQ: e.matmul(
                            ps_[:, 0:NQ], lhsT=k2[:, c, kb * 128:(kb + 1) * 128], rhs=q2[:, c, qb0 * 128:qb0 * 128 + NQ],
                            start=True, stop=True), reads=[kt2_b, qt2_b], writes=[psb_], mark=True)
                        tbl = 0 if kb < 32 else 1
                        bo = (tbl * 6 + h) * 67 + (32 + qb0 - kb) + 3
                        pt, pt_b = ptd_ring.next()
                        P.op("scalar", lambda e, pt=pt, ps_=ps_, bo=bo, NQ=NQ: e.activation(
                            out=pt[:, 0:NQ], in_=ps_[:, 0:NQ], func=AF.Exp, scale=SC, bias=dbias[:, bo:bo + 1]),
                            reads=[psb_, dbias_b], writes=[pt_b])
                        for qi in range(G):
                            qb = 32 + qb0 + qi
                            if kb > qb:
                                continue
                            if kb == qb:
                                P.op("gpsimd", lambda e, pt=pt, qi=qi: e.tensor_tensor(
                                    out=pt[:, qi * 128:(qi + 1) * 128], in0=pt[:, qi * 128:(qi + 1) * 128], in1=tri, op=ALU.mult),
                                    reads=[pt_b, tri_b], writes=[pt_b])
                            ac_, acb_ = accs[qi]
                            P.pe(lambda e, ac_=ac_, pt=pt, qi=qi, kb=kb, qb=qb: e.matmul(
                                ac_[:, 0:257], lhsT=pt[:, qi * 128:(qi + 1) * 128], rhs=vd3[:, kb, 0:257],
                                start=(kb == 0), stop=(kb == qb)), reads=[pt_b, vd_b], writes=[acb_], mark=(kb == qb))
                    for qi in range(G):
                        ot, ot_b = ot_ring.next()
                        ac_, acb_ = accs[qi]
                        P.op("vector", lambda e, ot=ot, ac_=ac_: e.tensor_copy(out=ot[:, 0:257], in_=ac_[:, 0:257]),
                             reads=[acb_], writes=[ot_b])
                        ots[c][qi] = (ot, ot_b)
                for qi in range(G):
                    (o1, o1_b), (o2, o2_b) = ots[0][qi], ots[1][qi]
                    rc, rc_b = rc_ring.next()
                    yb, yb_b = yb_ring.next()
                    ybt, ybt_b = ybt_ring.next()
                    P.op("vector", lambda e, rc=rc, o1=o1: e.reciprocal(out=rc[:, 0:1], in_=o1[:, 256:257]),
                         reads=[o1_b], writes=[rc_b])
                    P.op("vector", lambda e, rc=rc, o2=o2: e.reciprocal(out=rc[:, 1:2], in_=o2[:, 256:257]),
                         reads=[o2_b], writes=[rc_b])
                    P.op("vector", lambda e, rc=rc: e.tensor_tensor(out=rc[:, 2:3], in0=rc[:, 1:2], in1=neglam, op=ALU.mult),
                         reads=[rc_b, small_b], writes=[rc_b])
                    P.op("vector", lambda e, rc=rc: e.memset(rc[:, 3:4], 0.0), writes=[rc_b])
                    P.op("vector", lambda e, rc=rc, o1=o1: e.tensor_scalar(out=o1[:, 0:256], in0=o1[:, 0:256], scalar1=rc[:, 0:1],
                                                                           scalar2=None, op0=ALU.mult),
                         reads=[rc_b, o1_b], writes=[o1_b])
                    P.op("vector", lambda e, rc=rc, o1=o1, o2=o2: e.scalar_tensor_tensor(
                        out=o2[:, 0:256], in0=o2[:, 0:256], scalar=rc[:, 2:3], in1=o1[:, 0:256], op0=ALU.mult, op1=ALU.add),
                        reads=[rc_b, o1_b, o2_b], writes=[o2_b])
                    P.op("scalar", lambda e, rc=rc, o1=o1, o2=o2: e.activation(out=o1[:, 0:256], in_=o2[:, 0:256], func=AF.Square,
                                                                               accum_out=rc[:, 3:4]),
                         reads=[o2_b, rc_b], writes=[o1_b, rc_b])
                    P.op("scalar", lambda e, rc=rc: e.activation(out=rc[:, 4:5], in_=rc[:, 3:4], func=AF.Ln, scale=1.0 / 256,
                                                                 bias=eps_c), reads=[rc_b, small_b], writes=[rc_b])
                    P.op("scalar", lambda e, rc=rc: e.activation(out=rc[:, 4:5], in_=rc[:, 4:5], func=AF.Exp, scale=-0.5),
                         reads=[rc_b], writes=[rc_b])
                    P.op("vector", lambda e, rc=rc, o2=o2, yb=yb, zb=zb, qi=qi: e.scalar_tensor_tensor(
                        out=yb, in0=o2[:, 0:256], scalar=rc[:, 4:5], in1=zb[:, qi * 256:(qi + 1) * 256], op0=ALU.mult,
                        op1=ALU.mult), reads=[rc_b, o2_b, zb_b], writes=[yb_b])
                    ps_, psb_ = next_bank(4, 8)
                    pv = ps_.bitcast(BF16)
                    for k in range(2):
                        P.pe(lambda e, k=k, pv=pv, yb=yb: e.transpose(pv[:, k * 128:(k + 1) * 128], yb[:, k * 128:(k + 1) * 128],
                                                                      ident),
                             reads=[yb_b, ident_b], writes=[psb_], mark=(k == 1))
                    P.op("scalar", lambda e, ybt=ybt, pv=pv: e.copy(out=ybt, in_=pv[:, 0:256]), reads=[psb_], writes=[ybt_b])
                    t0 = (qb0 + qi) * 128
                    dma("scalar", YT[8 + 2 * h:10 + 2 * h, :, t0:t0 + 128].rearrange("k p t -> p k t"),
                        ybt.rearrange("p (k t) -> p k t", k=2), reads=[ybt_b], sem_buf=ybt_b)
        barrier()
        ar.off = persist_off
        if STOP == 'DIFF':
            P.emit()
            return nc

        wo_ring = ar.ring_bf16(KC * 512, 3, "wo")
        ytt, ytt_b = ar.bf16(20 * 512, "ytt")
        ytt3 = ytt.rearrange("p (k t) -> p k t", k=20)
        mt, mt_b = ar.bf16(KC * 512, "mt")
        mt3 = mt.rearrange("p (k t) -> p k t", k=KC)
        gl_ring = ar.ring_f32(512, 4, "gl")
        tp_ring = ar.ring_f32(512, 3, "tp")
        xo_ring = ar.ring_f32(D, 4, "xo")
        fs_ring = ar.ring_f32(4, 4, "fs")
        ar_junk = ar.bf16(D, "junk")

        def load_wo(src, nk, col0):
            wt, wt_b = wo_ring.next()
            w3 = wt.rearrange("p (k n) -> p k n", k=KC)
            dma("sync", w3[:, 0:nk, :], src[:, col0:col0 + 512].rearrange("(kc p) n -> p kc n", p=128), writes=[wt_b],
                sem_buf=wt_b)
            return w3, wt_b

        for tg in range(8):
            T0 = tg * 512
            dma("sync", ytt3, YT[:, :, T0:T0 + 512].rearrange("k p t -> p k t"), writes=[ytt_b], sem_buf=ytt_b)
            for fo4 in range(4):
                wd3, wd_b = load_wo(wbf_od, 8, fo4 * 512)
                wf3, wf_b = load_wo(wbf_of, 12, fo4 * 512)
                for fb in range(4):
                    fo = fo4 * 4 + fb
                    pa_, pab_ = next_bank(0, 8)
                    for kc in range(8):
                        P.pe(lambda e, pa_=pa_, wd3=wd3, kc=kc, fb=fb: e.matmul(
                            pa_, lhsT=wd3[:, kc, fb * 128:(fb + 1) * 128], rhs=ytt3[:, kc, :], start=(kc == 0), stop=(kc == 7)),
                            reads=[wd_b, ytt_b], writes=[pab_], mark=(kc == 7))
                    pb_, pbb_ = next_bank(0, 8)
                    for kc in range(12):
                        P.pe(lambda e, pb_=pb_, wf3=wf3, kc=kc, fb=fb: e.matmul(
                            pb_, lhsT=wf3[:, kc, fb * 128:(fb + 1) * 128], rhs=ytt3[:, 8 + kc, :], start=(kc == 0),
                            stop=(kc == 11)), reads=[wf_b, ytt_b], writes=[pbb_], mark=(kc == 11))
                    ga, ga_b = gl_ring.next()
                    gb, gb_b = gl_ring.next()
                    dma("sync", ga, GL[fo, :, T0:T0 + 512], writes=[ga_b], sem_buf=ga_b)
                    dma("sync", gb, GL[16 + fo, :, T0:T0 + 512], writes=[gb_b], sem_buf=gb_b)
                    tp, tp_b = tp_ring.next()
                    P.op("vector", lambda e, tp=tp, pa_=pa_, ga=ga: e.tensor_tensor(out=tp, in0=pa_, in1=ga, op=ALU.mult),
                         reads=[pab_, ga_b], writes=[tp_b])
                    P.op("vector", lambda e, gb=gb, pb_=pb_: e.tensor_tensor(out=gb, in0=pb_, in1=gb, op=ALU.mult),
                         reads=[pbb_, gb_b], writes=[gb_b])
                    P.op("gpsimd", lambda e, tp=tp, gb=gb, fo=fo: e.tensor_tensor(out=mt3[:, fo, :], in0=tp, in1=gb, op=ALU.add),
                         reads=[tp_b, gb_b], writes=[mt_b])
            xos = []
            for tt in range(4):
                xo, xo_b = xo_ring.next()
                t0 = CTX + T0 + tt * 128
                dma("sync", xo, x[t0:t0 + 128, :], writes=[xo_b], sem_buf=xo_b)
                xos.append((xo, xo_b))
            for c4 in range(4):
                ww3, ww_b = load_wo(wbf_out, 16, c4 * 512)
                for tt in range(4):
                    xo, xo_b = xos[tt]
                    po_, pob_ = next_bank(0, 8)
                    for kc in range(KC):
                        P.pe(lambda e, po_=po_, ww3=ww3, kc=kc, tt=tt: e.matmul(
                            po_, lhsT=mt3[:, kc, tt * 128:(tt + 1) * 128], rhs=ww3[:, kc, :], start=(kc == 0),
                            stop=(kc == KC - 1)), reads=[ww_b, mt_b], writes=[pob_], mark=(kc == KC - 1))
                    tp, tp_b = tp_ring.next()
                    P.op("vector", lambda e, tp=tp, po_=po_, c4=c4: e.tensor_tensor(out=tp, in0=po_, in1=G_rep[:, c4 * 512:(c4 + 1) * 512],
                                                                                    op=ALU.mult),
                         reads=[pob_, G_b], writes=[tp_b])
                    P.op("gpsimd", lambda e, tp=tp, xo=xo, c4=c4: e.tensor_tensor(
                        out=xo[:, c4 * 512:(c4 + 1) * 512], in0=xo[:, c4 * 512:(c4 + 1) * 512], in1=tp, op=ALU.add),
                        reads=[tp_b, xo_b], writes=[xo_b])
            for tt in range(4):
                xo, xo_b = xos[tt]
                fs, fs_b = fs_ring.next()
                t0 = T0 + tt * 128
                P.op("vector", lambda e, fs=fs: e.memset(fs, 0.0), writes=[fs_b])
                tp2, tp2_b = ar_junk
                P.op("scalar", lambda e, xo=xo, fs=fs, tp2=tp2: e.activation(out=tp2, in_=xo, func=AF.Square, accum_out=fs[:, 0:1]),
                     reads=[xo_b], writes=[tp2_b, fs_b])
                P.op("scalar", lambda e, fs=fs: e.activation(out=fs[:, 1:2], in_=fs[:, 0:1], func=AF.Ln, scale=1.0 / D, bias=eps_c),
                     reads=[fs_b, small_b], writes=[fs_b])
                P.op("scalar", lambda e, fs=fs: e.activation(out=fs[:, 1:2], in_=fs[:, 1:2], func=AF.Exp, scale=-0.5),
                     reads=[fs_b], writes=[fs_b])
                P.op("vector", lambda e, xo=xo, fs=fs: e.scalar_tensor_tensor(out=xo, in0=xo, scalar=fs[:, 1:2], in1=GF_rep,
                                                                              op0=ALU.mult, op1=ALU.mult),
                     reads=[xo_b, fs_b, GF_b], writes=[xo_b])
                dma("sync", y_out[t0:t0 + 128, :], xo, reads=[xo_b], sem_buf=xo_b)
        barrier()
        P.emit()
    return nc


def _alibi(n):
    return (2.0 ** (-8.0 * np.arange(1, n + 1) / n)).astype(np.float64)


def _tables(h):
    ctx_valid = 1.0 if h == 1 else 0.0
    ctx_bias = 0.0 if h == 1 else NEG
    j = np.arange(128, dtype=np.float64)
    sl_d = _alibi(6)
    dbias = np.zeros((128, 2, 6, 67), np.float64)
    for hd in range(6):
        nq = 256 if hd == 0 else 512
        for di in range(67):
            dist = di - 3
            b = sl_d[hd] * (j - 128.0 * dist - nq / 2)
            dbias[:, 1, hd, di] = b
            dbias[:, 0, hd, di] = b + ctx_bias
    dbias = np.maximum(dbias, -60000.0)
    jj = j[:, None]
    ii = j[None, :]
    tri = (jj <= ii).astype(np.float32)
    ident = np.eye(128, dtype=np.float32)
    sl_a = _alibi(24)
    mdil = np.zeros((128, 24, 3, 128), np.float64)
    for hh in range(24):
        r = DIL[hh // 8][1]
        se = sl_a[hh] * r
        cur = (jj <= ii) * np.exp(-se * np.maximum(ii - jj, 0))
        prev = (jj >= ii) * np.exp(-se * (128 + ii - jj))
        mdil[:, hh, 0, :] = ctx_valid * prev
        mdil[:, hh, 1, :] = cur
        mdil[:, hh, 2, :] = prev
    return {
        "dbias": np.ascontiguousarray(dbias.reshape(128, -1).astype(np.float32)),
        "tri": tri,
        "ident": ident,
        "mdil": np.ascontiguousarray(mdil.reshape(128, -1).astype(np.float32)),
    }


def _in_maps(x, c, w_ada, b_ada, g_norm, w_in, w_o_dil, w_o_diff, w_out, lambda_q1, lambda_k1, lambda_q2, lambda_k2,
             g_subln, g_final):
    f = lambda a: np.ascontiguousarray(np.asarray(a, dtype=np.float32))
    x = f(x)
    shared = {
        "w_ada": f(w_ada)[0], "b_ada": f(b_ada)[0], "g_norm": f(g_norm)[0], "w_in": f(w_in)[0],
        "w_o_dil": f(w_o_dil)[0], "w_o_diff": f(w_o_diff)[0], "w_out": f(w_out)[0],
        "lamv": np.concatenate([f(lambda_q1)[0], f(lambda_k1)[0], f(lambda_q2)[0], f(lambda_k2)[0]]),
        "g_subln": f(g_subln)[0], "g_final": f(g_final),
    }
    tabs = [_tables(0), _tables(1)]
    maps = []
    for core in range(8):
        b, h = core // 2, core % 2
        own = x[b, h * S_OWN:(h + 1) * S_OWN]
        ctx = x[b, 0:CTX] if h == 1 else np.zeros((CTX, D), np.float32)
        m = dict(shared)
        m["x"] = np.ascontiguousarray(np.concatenate([ctx, own], axis=0))
        m["c"] = np.ascontiguousarray(f(c)[b])
        m.update(tabs[h])
        maps.append(m)
    return maps


_NC = None


def kernel(**inputs):
    global _NC
    if _NC is None:
        _NC = build_program()
    maps = _in_maps(**inputs)
    res = run_bass_kernel_spmd(_NC, maps, core_ids=list(range(8)))
    out = np.zeros((4, 8192, D), np.float32)
    for core in range(8):
        b, h = core // 2, core % 2
        out[b, h * S_OWN:(h + 1) * S_OWN] = res.results[core]["y"]
    return out
```
